# Optimizing a Trainium2 kernel written in Bass

```python
import math
import jax, jax.numpy as jnp
from jax import lax
import numpy as np

D_MODEL = 2048
BATCH = 4
SEQ = 8192
DEPTH = 4

CHUNK = 64
N_MIXERS = 3
EPS = 1e-6
MASK_VALUE = -1e30

ATT_HEADS = 16
ATT_HEAD_DIM = D_MODEL // ATT_HEADS
LEFT_CHUNKS = 8
BAND_LEFT = LEFT_CHUNKS * CHUNK
BAND = (LEFT_CHUNKS + 1) * CHUNK
MAX_REL = 256
NUM_REL = (CHUNK - 1) + MAX_REL + 1

POOL_WINDOWS = (2, 4, 8, 16)
POOL_GROUPS = len(POOL_WINDOWS)
POOL_DIM = D_MODEL // POOL_GROUPS

GDN_K_HEADS = 16
GDN_V_HEADS = 32
GDN_K_DIM = D_MODEL // GDN_K_HEADS
GDN_V_DIM = D_MODEL // GDN_K_HEADS
GDN_KEY = GDN_K_HEADS * GDN_K_DIM
GDN_VAL = GDN_V_HEADS * GDN_V_DIM
GDN_CONV = 4
GDN_CONV_CH = 2 * GDN_KEY + GDN_VAL
GDN_IN = GDN_CONV_CH + GDN_VAL + 2 * GDN_V_HEADS

D_FF = 128 * ((8 * D_MODEL // 3 + 127) // 128)
FFN_CONV = 3

N_ATT_LAYERS = (DEPTH + 2) // 3
N_POOL_LAYERS = (DEPTH + 1) // 3
N_GDN_LAYERS = DEPTH // 3

kernel_name = "hybrid_chunk_causal_encoder"


def rms_norm(x, gain):
    xf = x.astype(jnp.float32)
    y = xf * lax.rsqrt(jnp.mean(xf * xf, axis=-1, keepdims=True) + EPS)
    return (y * gain.astype(jnp.float32)).astype(x.dtype)


def l2_norm(x):
    xf = x.astype(jnp.float32)
    return xf * lax.rsqrt(jnp.sum(xf * xf, axis=-1, keepdims=True) + EPS)


def causal_depthwise_conv(x, w):
    k = w.shape[0]
    s = x.shape[1]
    xp = jnp.pad(x, ((0, 0), (k - 1, 0), (0, 0)))
    out = xp[:, 0:s] * w[0]
    for j in range(1, k):
        out = out + xp[:, j:j + s] * w[j]
    return out


def chunk_band_attention(h, w_qkv, q_gain, k_gain, rel_bias, w_o):
    b, s, d = h.shape
    nc = s // CHUNK
    qkv = h @ w_qkv
    q = qkv[..., :d].reshape(b, s, ATT_HEADS, ATT_HEAD_DIM)
    k = qkv[..., d:2 * d].reshape(b, s, ATT_HEADS, ATT_HEAD_DIM)
    v = qkv[..., 2 * d:].reshape(b, s, ATT_HEADS, ATT_HEAD_DIM)
    q = rms_norm(q, q_gain)
    k = rms_norm(k, k_gain)
    k_pad = jnp.pad(k, ((0, 0), (BAND_LEFT, 0), (0, 0), (0, 0)))
    v_pad = jnp.pad(v, ((0, 0), (BAND_LEFT, 0), (0, 0), (0, 0)))
    rel = BAND_LEFT + jnp.arange(CHUNK)[:, None] - jnp.arange(BAND)[None, :]
    rel_idx = jnp.clip(rel, -(CHUNK - 1), MAX_REL) + (CHUNK - 1)
    bias = rel_bias.astype(jnp.float32)[:, rel_idx]
    scale = ATT_HEAD_DIM ** -0.5
    q_chunks = q.reshape(b, nc, CHUNK, ATT_HEADS, ATT_HEAD_DIM).transpose(1, 0, 2, 3, 4)

    def per_chunk(args):
        c, qc = args
        kb = lax.dynamic_slice_in_dim(k_pad, c * CHUNK, BAND, axis=1)
        vb = lax.dynamic_slice_in_dim(v_pad, c * CHUNK, BAND, axis=1)
        sc = jnp.einsum('bqhd,bkhd->bhqk', qc, kb,
                        preferred_element_type=jnp.float32) * scale + bias
        valid = (c * CHUNK - BAND_LEFT + jnp.arange(BAND)) >= 0
        sc = jnp.where(valid[None, None, None, :], sc, MASK_VALUE)
        p = jax.nn.softmax(sc, axis=-1).astype(vb.dtype)
        return jnp.einsum('bhqk,bkhd->bqhd', p, vb)

    o = lax.map(per_chunk, (jnp.arange(nc, dtype=jnp.int32), q_chunks))
    o = o.transpose(1, 0, 2, 3, 4).reshape(b, s, d)
    return o @ w_o


def multiscale_pool_mixer(h, pool_w, pool_scale):
    b, s, d = h.shape
    hf = h.astype(jnp.float32)
    cs = jnp.cumsum(jnp.pad(hf, ((0, 0), (1, 0), (0, 0))), axis=1)
    pos = jnp.arange(s)
    groups = []
    for g, w in enumerate(POOL_WINDOWS):
        csg = cs[..., g * POOL_DIM:(g + 1) * POOL_DIM]
        upper = csg[:, 1:]
        lower = jnp.pad(csg[:, :s + 1 - w], ((0, 0), (w - 1, 0), (0, 0)))
        count = jnp.minimum(pos + 1, w).astype(jnp.float32)[None, :, None]
        groups.append((upper - lower) / count - hf[..., g * POOL_DIM:(g + 1) * POOL_DIM])
    pooled = jnp.stack(groups, axis=2)
    y = jnp.einsum('bsgc,gce->bsge', pooled, pool_w.astype(jnp.float32)).reshape(b, s, d)
    return (y * pool_scale.astype(jnp.float32)).astype(h.dtype)


def gated_delta_rule(q, k, v, g, beta):
    b, s, h, dk = q.shape
    dv = v.shape[-1]
    nc = s // CHUNK

    def to_chunks(t):
        return t.reshape(b, nc, CHUNK, h, -1).transpose(0, 3, 1, 2, 4)

    q, k, v = to_chunks(q), to_chunks(k), to_chunks(v)
    g = g.reshape(b, nc, CHUNK, h).transpose(0, 3, 1, 2)
    beta = beta.reshape(b, nc, CHUNK, h).transpose(0, 3, 1, 2)
    gc = jnp.cumsum(g, axis=-1)
    idx = jnp.arange(CHUNK)
    causal = idx[:, None] >= idx[None, :]
    strict = idx[:, None] > idx[None, :]
    diff = gc[..., :, None] - gc[..., None, :]
    decay = jnp.where(causal, jnp.exp(jnp.where(causal, diff, 0.0)), 0.0)
    kb = k * beta[..., None]
    vb = v * beta[..., None]
    a_strict = jnp.where(strict, jnp.einsum('bhncd,bhnjd->bhncj', kb, k) * decay, 0.0)
    eye = jnp.eye(CHUNK, dtype=jnp.float32)
    rhs = jnp.concatenate([vb, kb * jnp.exp(gc)[..., None]], axis=-1)
    sol = lax.linalg.triangular_solve(a_strict + eye, rhs, left_side=True, lower=True,
                                      unit_diagonal=True)
    u = sol[..., :dv]
    w = sol[..., dv:]
    attn = jnp.einsum('bhncd,bhnjd->bhncj', q, k) * decay
    qg = q * jnp.exp(gc)[..., None]
    k_state = k * jnp.exp(gc[..., -1:] - gc)[..., None]
    chunk_decay = jnp.exp(gc[..., -1])
    xs = tuple(jnp.moveaxis(t, 2, 0) for t in (qg, attn, u, w, k_state, chunk_decay))

    def step(state, inp):
        qg_c, attn_c, u_c, w_c, ks_c, dec_c = inp
        v_new = u_c - jnp.einsum('bhcd,bhde->bhce', w_c, state)
        o_c = (jnp.einsum('bhcd,bhde->bhce', qg_c, state)
               + jnp.einsum('bhcj,bhje->bhce', attn_c, v_new))
        state = state * dec_c[..., None, None] + jnp.einsum('bhcd,bhce->bhde', ks_c, v_new)
        return state, o_c

    state0 = jnp.zeros((b, h, dk, dv), jnp.float32)
    _, o = lax.scan(step, state0, xs)
    return o.transpose(1, 0, 3, 2, 4).reshape(b, s, h, dv)


def gated_deltanet_mixer(h, w_in, conv_w, a_log, dt_bias, o_gain, w_o):
    b, s, _ = h.shape
    proj = h @ w_in
    qkv = jax.nn.silu(causal_depthwise_conv(proj[..., :GDN_CONV_CH], conv_w))
    gate = proj[..., GDN_CONV_CH:GDN_CONV_CH + GDN_VAL]
    a = proj[..., GDN_CONV_CH + GDN_VAL:GDN_CONV_CH + GDN_VAL + GDN_V_HEADS]
    bt = proj[..., GDN_CONV_CH + GDN_VAL + GDN_V_HEADS:]
    q = qkv[..., :GDN_KEY].reshape(b, s, GDN_K_HEADS, GDN_K_DIM)
    k = qkv[..., GDN_KEY:2 * GDN_KEY].reshape(b, s, GDN_K_HEADS, GDN_K_DIM)
    v = qkv[..., 2 * GDN_KEY:].reshape(b, s, GDN_V_HEADS, GDN_V_DIM).astype(jnp.float32)
    q = l2_norm(q) * (GDN_K_DIM ** -0.5)
    k = l2_norm(k)
    rep = GDN_V_HEADS // GDN_K_HEADS
    q = jnp.repeat(q, rep, axis=2)
    k = jnp.repeat(k, rep, axis=2)
    beta = jax.nn.sigmoid(bt.astype(jnp.float32))
    g = -jnp.exp(a_log.astype(jnp.float32)) * jax.nn.softplus(
        a.astype(jnp.float32) + dt_bias.astype(jnp.float32))
    o = gated_delta_rule(q, k, v, g, beta)
    o = rms_norm(o, o_gain) * jax.nn.silu(
        gate.astype(jnp.float32).reshape(b, s, GDN_V_HEADS, GDN_V_DIM))
    return o.reshape(b, s, GDN_VAL).astype(h.dtype) @ w_o


def conv_ffn(h, w_up, conv_w, w_down):
    up = h @ w_up
    u = causal_depthwise_conv(up[..., :D_FF], conv_w)
    return (jax.nn.silu(u) * up[..., D_FF:]) @ w_down


def _normal(k, shape, scale):
    return scale * jax.random.normal(k, shape, jnp.float32)


def setup_inputs(seed: int = 0) -> dict:
    key = jax.random.key(seed)
    ks = jax.random.split(key, 24)
    d = D_MODEL
    dt = jnp.exp(jax.random.uniform(ks[14], (N_GDN_LAYERS, GDN_V_HEADS), jnp.float32,
                                    minval=math.log(1e-3), maxval=math.log(1e-1)))
    return {
        "x": _normal(ks[0], (BATCH, SEQ, d), 1.0),
        "mix_norm": 1.0 + _normal(ks[1], (DEPTH, d), 0.02),
        "ffn_norm": 1.0 + _normal(ks[2], (DEPTH, d), 0.02),
        "att_w_qkv": _normal(ks[3], (N_ATT_LAYERS, d, 3 * d), d ** -0.5),
        "att_q_gain": 1.0 + _normal(ks[4], (N_ATT_LAYERS, ATT_HEAD_DIM), 0.02),
        "att_k_gain": 1.0 + _normal(ks[5], (N_ATT_LAYERS, ATT_HEAD_DIM), 0.02),
        "att_rel_bias": _normal(ks[6], (N_ATT_LAYERS, ATT_HEADS, NUM_REL), 0.2),
        "att_w_o": _normal(ks[7], (N_ATT_LAYERS, d, d), d ** -0.5),
        "pool_w": _normal(ks[8], (N_POOL_LAYERS, POOL_GROUPS, POOL_DIM, POOL_DIM),
                           POOL_DIM ** -0.5),
        "pool_scale": 1.0 + _normal(ks[9], (N_POOL_LAYERS, d), 0.1),
        "gdn_w_in": _normal(ks[10], (N_GDN_LAYERS, d, GDN_IN), d ** -0.5),
        "gdn_conv": _normal(ks[11], (N_GDN_LAYERS, GDN_CONV, GDN_CONV_CH), GDN_CONV ** -0.5),
        "gdn_a_log": jnp.log(jax.random.uniform(ks[12], (N_GDN_LAYERS, GDN_V_HEADS),
                                                 jnp.float32, minval=1.0, maxval=16.0)),
        "gdn_dt_bias": dt + jnp.log(-jnp.expm1(-dt)),
        "gdn_o_gain": 1.0 + _normal(ks[13], (N_GDN_LAYERS, GDN_V_DIM), 0.02),
        "gdn_w_o": _normal(ks[15], (N_GDN_LAYERS, GDN_VAL, d), GDN_VAL ** -0.5),
        "ffn_w_up": _normal(ks[16], (DEPTH, d, 2 * D_FF), d ** -0.5),
        "ffn_conv": _normal(ks[17], (DEPTH, FFN_CONV, D_FF), FFN_CONV ** -0.5),
        "ffn_w_down": _normal(ks[18], (DEPTH, D_FF, d), D_FF ** -0.5),
    }


def reference(x, mix_norm, ffn_norm, att_w_qkv, att_q_gain, att_k_gain, att_rel_bias,
              att_w_o, pool_w, pool_scale, gdn_w_in, gdn_conv, gdn_a_log, gdn_dt_bias,
              gdn_o_gain, gdn_w_o, ffn_w_up, ffn_conv, ffn_w_down):
    for i in range(DEPTH):
        kind = i % N_MIXERS
        j = i // N_MIXERS
        h = rms_norm(x, mix_norm[i])
        if kind == 0:
            y = chunk_band_attention(h, att_w_qkv[j], att_q_gain[j], att_k_gain[j],
                                     att_rel_bias[j], att_w_o[j])
        elif kind == 1:
            y = multiscale_pool_mixer(h, pool_w[j], pool_scale[j])
        else:
            y = gated_deltanet_mixer(h, gdn_w_in[j], gdn_conv[j], gdn_a_log[j],
                                     gdn_dt_bias[j], gdn_o_gain[j], gdn_w_o[j])
        x = x + y
        h = rms_norm(x, ffn_norm[i])
        x = x + conv_ffn(h, ffn_w_up[i], ffn_conv[i], ffn_w_down[i])
    return x
```

```python
import os
import numpy as np
from contextlib import ExitStack
import ml_dtypes
import concourse.bass as bass
import concourse.mybir as mybir
from concourse.bass_utils import run_bass_kernel_spmd

F32 = mybir.dt.float32
BF16 = mybir.dt.bfloat16
AF = mybir.ActivationFunctionType
ALU = mybir.AluOpType
AX = mybir.AxisListType
NPBF = ml_dtypes.bfloat16


class Dep:
    __slots__ = ("w", "r", "excl")

    def __init__(self, excl=False):
        self.w = None
        self.r = {}
        self.excl = excl


class Prog:
    def __init__(self):
        self.nc = bass.Bass("TRN2", target_bir_lowering=False)
        self.es = ExitStack()
        nc = self.nc
        self.e = {"pe": nc.tensor, "act": nc.scalar, "dve": nc.vector,
                  "pool": nc.gpsimd, "sp": nc.sync}
        self.semh = {}
        self.cnt = {}
        self.known = {k: {} for k in self.e}
        self.final = []
        self.log = {k: [] for k in self.e}

    def sem(self, name):
        if name not in self.semh:
            self.semh[name] = self.es.enter_context(self.nc.semaphore(name))
            self.cnt[name] = 0
        return self.semh[name]

    def din(self, name, shape, dt):
        return self.nc.dram_tensor(name, list(shape), dt, kind="ExternalInput").ap()

    def dout(self, name, shape, dt):
        return self.nc.dram_tensor(name, list(shape), dt, kind="ExternalOutput").ap()

    def dscratch(self, name, shape, dt):
        return self.nc.dram_tensor(name, list(shape), dt, kind="Internal").ap()

    def sb(self, name, shape, dt):
        return self.es.enter_context(self.nc.sbuf_tensor(name, list(shape), dt))

    def ps(self, name, shape, dt=F32):
        return self.es.enter_context(self.nc.psum_tensor(name, list(shape), dt))

    def wait(self, eng, tok):
        if tok is None:
            return
        s, v = tok
        import os
        if os.environ.get("NOSELF") and s == "e_" + eng:
            return
        if self.known[eng].get(s, 0) >= v:
            return
        self.e[eng].wait_ge(self.semh[s], v)
        self.known[eng][s] = v
        self.log[eng].append(("w", s, v))

    def _deps(self, eng, reads, writes):
        own = "e_" + eng
        for d in reads:
            self.wait(eng, d.w)
            if d.excl:
                for s, v in d.r.items():
                    if s != own:
                        self.wait(eng, (s, v))
        for d in writes:
            if d.w is not None and not (d.w[0] == own and (eng == "pe" or d.w[1] > self.cnt.get(own, 0))):
                self.wait(eng, d.w)
            for s, v in d.r.items():
                if s != own:
                    self.wait(eng, (s, v))

    def _mark(self, tok, reads, writes):
        s, v = tok
        for d in reads:
            if d.r.get(s, 0) < v:
                d.r[s] = v
        for d in writes:
            d.w = tok
            d.r = {}

    def op(self, eng, fn, reads=(), writes=(), inc=True):
        self._deps(eng, reads, writes)
        ins = fn(self.e[eng])
        s = "e_" + eng
        self.sem(s)
        if inc:
            ins.then_inc(self.semh[s], 1)
            self.cnt[s] += 1
            tok = (s, self.cnt[s])
            self.log[eng].append(("i", s, 1))
        else:
            tok = (s, self.cnt[s] + 1)
        self._mark(tok, reads, writes)
        return tok

    def dma(self, q, out_ap, in_ap, sem, reads=(), writes=(), final=False):
        if sem == "s_c":
            sem = f"s_c{len(self.semh)}"
        self._deps(q, reads, writes)
        ins = self.e[q].dma_start(out=out_ap, in_=in_ap)
        self.sem(sem)
        ins.then_inc(self.semh[sem], 16)
        self.cnt[sem] += 16
        self.log[q].append(("i", sem, 16))
        tok = (sem, self.cnt[sem])
        self._mark(tok, reads, writes)
        if final:
            self.final.append(tok)
        return tok

    def sync_all(self, deps):
        for eng in self.e:
            for d in deps:
                self.wait(eng, d.w)

    def simulate(self):
        val = {k: 0 for k in self.semh}
        pc = {k: 0 for k in self.log}
        prog = True
        while prog:
            prog = False
            for k, lst in self.log.items():
                while pc[k] < len(lst):
                    kind, s, v = lst[pc[k]]
                    if kind == "w":
                        if val[s] < v:
                            break
                    else:
                        val[s] += v
                    pc[k] += 1
                    prog = True
        stuck = {k: (pc[k], len(l), l[pc[k]]) for k, l in self.log.items() if pc[k] < len(l)}
        assert not stuck, f"DEADLOCK: {stuck}"

    def finish(self, eng="sp"):
        for tok in self.final:
            self.wait(eng, tok)
        import os
        if os.environ.get("STRICT_END"):
            for sname in ("e_pe", "e_act", "e_dve", "e_pool"):
                if sname in self.cnt and (os.environ["STRICT_END"] == "all" or sname in os.environ["STRICT_END"]):
                    self.wait(eng, (sname, self.cnt[sname]))
        self.simulate()
        self.es.close()
        return self.nc


EPS = 1e-6


def rms_cols(P, x_of, n, h_of, gain, ones, ps_r, d_ps_r, sq_tiles, rstd, d_rstd, d_x, d_h, DC, D):
    for c in range(DC):
        sq, d_sq = sq_tiles[c % len(sq_tiles)]
        P.op("act", lambda e: e.activation(out=sq[:, :n], in_=x_of(c), func=AF.Square),
             reads=[d_x], writes=[d_sq])
        P.op("pe", lambda e: e.matmul(ps_r[:, :n], ones[:], sq[:, :n], start=(c == 0), stop=(c == DC - 1)),
             reads=[d_sq], writes=[d_ps_r], inc=(c == DC - 1) or True)
    P.op("act", lambda e: e.activation(out=rstd[:, :n], in_=ps_r[:, :n], func=AF.Sqrt, bias=EPS, scale=1.0),
         reads=[d_ps_r], writes=[d_rstd])
    P.op("dve", lambda e: e.reciprocal(rstd[:, :n], rstd[:, :n]), reads=[d_rstd], writes=[d_rstd])
    for c in range(DC):
        P.op("dve", lambda e: e.scalar_tensor_tensor(out=h_of(c), in0=x_of(c), scalar=gain[:, c:c + 1],
                                                      in1=rstd[:, :n], op0=ALU.mult, op1=ALU.mult),
             reads=[d_x, d_rstd], writes=[d_h])


def build_ffn(NT, DC=16, FC=43, TT=512, DBG=False):
    D = DC * 128
    P = Prog()
    ntok = NT * TT
    xT = P.din("xT", [DC, 128, 2 + ntok], F32)
    gain_d = P.din("gain", [128, DC], F32)
    cw_d = P.din("cw", [128, FC * 3], F32)
    wup = P.din("wup", [FC, 128, DC * 2 * 128], BF16)
    wdn = P.din("wdn", [DC, 128, FC * 128], BF16)
    yT = P.dout("yT", [DC, 128, ntok], F32)

    xs = P.sb("xs", [128, DC, TT], F32); d_xs = Dep()
    xh = P.sb("xh", [128, DC, 2], F32); d_xh = Dep()
    h = P.sb("h", [128, DC, TT], BF16); d_h = Dep()
    hh = P.sb("hh", [128, DC, 2], BF16); d_hh = Dep()
    gT = P.sb("gT", [128, FC, TT], BF16); d_g = [Dep() for _ in range(FC)]
    gain = P.sb("gain_s", [128, DC], F32); d_gain = Dep()
    cw = P.sb("cw_s", [128, FC * 3], F32); d_cw = Dep()
    ones = P.sb("ones", [128, 128], BF16); d_ones = Dep()
    carry = P.sb("carry", [128, FC, 2], F32); d_carry = [Dep() for _ in range(FC)]
    NUP, NDN = 3, 3
    upslab = [(P.sb(f"up{i}", [128, DC, 2, 128], BF16), Dep()) for i in range(NUP)]
    dnslab = [(P.sb(f"dn{i}", [128, FC, 128], BF16), Dep()) for i in range(NDN)]
    sq_tiles = [(P.sb(f"sq{i}", [128, TT], BF16), Dep()) for i in range(2)]
    rstd = P.sb("rstd", [128, TT], F32); d_rstd = Dep()
    uext = [(P.sb(f"uext{i}", [128, TT + 2], F32), Dep()) for i in range(2)]
    acc = [(P.sb(f"acc{i}", [128, TT], F32), Dep()) for i in range(2)]
    sil = [(P.sb(f"sil{i}", [128, TT], F32), Dep()) for i in range(2)]
    xres = [(P.sb(f"xres{i}", [128, TT], F32), Dep()) for i in range(2)]
    ys = [(P.sb(f"ys{i}", [128, TT], F32), Dep()) for i in range(2)]
    psU = [(P.ps(f"psU{i}", [128, TT]), Dep()) for i in range(2)]
    psV = [(P.ps(f"psV{i}", [128, TT]), Dep()) for i in range(2)]
    psY = [(P.ps(f"psY{i}", [128, TT]), Dep()) for i in range(2)]
    psR, d_psR = P.ps("psR", [128, TT]), Dep()

    P.dma("sp", gain[:], gain_d, "s_c", writes=[d_gain])
    P.dma("sp", cw[:], cw_d, "s_c", writes=[d_cw])
    P.op("dve", lambda e: e.memset(ones[:], 1.0 / D), writes=[d_ones])
    xTv = xT.rearrange("c p t -> p c t")
    yTv = yT.rearrange("c p t -> p c t")

    def load_x(t):
        P.dma("sp", xs[:], xTv[:, :, 2 + t * TT: 2 + (t + 1) * TT], "s_x", writes=[d_xs])

    def load_up(t, j):
        g = t * FC + j
        s, d = upslab[g % NUP]
        P.dma("sp", s[:].rearrange("p a b c -> p (a b c)"), wup[j], f"s_up{g % NUP}", writes=[d])

    def load_dn(t, i):
        g = t * DC + i
        s, d = dnslab[g % NDN]
        P.dma("sp", s[:].rearrange("p a b -> p (a b)"), wdn[i], f"s_dn{g % NDN}", writes=[d])

    P.dma("sp", xh[:], xTv[:, :, 0:2], "s_xh", writes=[d_xh])
    load_x(0)
    load_up(0, 0)
    load_up(0, 1)
    rms_cols(P, lambda c: xh[:, c, :], 2, lambda c: hh[:, c, :], gain, ones, psR, d_psR, sq_tiles,
             rstd, d_rstd, d_xh, d_hh, DC, D)
    for t in range(NT):
        rms_cols(P, lambda c: xs[:, c, :], TT, lambda c: h[:, c, :], gain, ones, psR, d_psR, sq_tiles,
                 rstd, d_rstd, d_xs, d_h, DC, D)
        if t + 1 < NT:
            load_x(t + 1)
        for j in range(FC):
            g = t * FC + j
            if j + 2 < FC:
                load_up(t, j + 2)
            elif t + 1 < NT:
                load_up(t + 1, j + 2 - FC)
            if j == FC - 2:
                load_dn(t, 0)
            if j == FC - 1:
                load_dn(t, 1)
            slab, d_slab = upslab[g % NUP]
            pu, d_pu = psU[g % 2]
            pv, d_pv = psV[g % 2]
            ue, d_ue = uext[g % 2]
            ac, d_ac = acc[g % 2]
            si, d_si = sil[g % 2]
            if t == 0:
                for k in range(DC):
                    P.op("pe", lambda e: e.matmul(pu[:, 0:2], slab[:, k, 0, :], hh[:, k, :],
                                                  start=(k == 0), stop=(k == DC - 1)),
                         reads=[d_slab, d_hh], writes=[d_pu], inc=(k == DC - 1))
                P.op("act", lambda e: e.copy(out=ue[:, 0:2], in_=pu[:, 0:2]), reads=[d_pu], writes=[d_ue])
            else:
                P.op("pool", lambda e: e.tensor_copy(ue[:, 0:2], carry[:, j, :]), reads=[d_carry[j]], writes=[d_ue])
            for k in range(DC):
                P.op("pe", lambda e: e.matmul(pu[:], slab[:, k, 0, :], h[:, k, :],
                                              start=(k == 0), stop=(k == DC - 1)),
                     reads=[d_slab, d_h], writes=[d_pu], inc=(k == DC - 1))
            for k in range(DC):
                P.op("pe", lambda e: e.matmul(pv[:], slab[:, k, 1, :], h[:, k, :],
                                              start=(k == 0), stop=(k == DC - 1)),
                     reads=[d_slab, d_h], writes=[d_pv], inc=(k == DC - 1))
            P.op("act", lambda e: e.copy(out=ue[:, 2:TT + 2], in_=pu[:]), reads=[d_pu], writes=[d_ue])
            P.op("pool", lambda e: e.tensor_copy(carry[:, j, :], ue[:, TT:TT + 2]), reads=[d_ue], writes=[d_carry[j]])
            P.op("dve", lambda e: e.tensor_scalar(ac[:], ue[:, 0:TT], cw[:, 3 * j:3 * j + 1], None, op0=ALU.mult),
                 reads=[d_ue, d_cw], writes=[d_ac])
            P.op("dve", lambda e: e.scalar_tensor_tensor(out=ac[:], in0=ue[:, 1:TT + 1], scalar=cw[:, 3 * j + 1:3 * j + 2],
                                                          in1=ac[:], op0=ALU.mult, op1=ALU.add),
                 reads=[d_ue, d_ac], writes=[d_ac])
            P.op("dve", lambda e: e.scalar_tensor_tensor(out=ac[:], in0=ue[:, 2:TT + 2], scalar=cw[:, 3 * j + 2:3 * j + 3],
                                                          in1=ac[:], op0=ALU.mult, op1=ALU.add),
                 reads=[d_ue, d_ac], writes=[d_ac])
            P.op("act", lambda e: e.activation(out=si[:], in_=ac[:], func=AF.Silu), reads=[d_ac], writes=[d_si])
            P.op("dve", lambda e: e.tensor_tensor(out=gT[:, j, :], in0=si[:], in1=pv[:], op=ALU.mult),
                 reads=[d_si, d_pv], writes=[d_g[j]])
        for i in range(DC):
            g = t * DC + i
            if i + 2 < DC:
                load_dn(t, i + 2)
            slab, d_slab = dnslab[g % NDN]
            py, d_py = psY[g % 2]
            xr, d_xr = xres[g % 2]
            yo, d_yo = ys[g % 2]
            P.dma("sp", xr[:], xT[i, :, 2 + t * TT: 2 + (t + 1) * TT], f"s_xr{g % 2}", writes=[d_xr])
            for j in range(FC):
                P.op("pe", lambda e: e.matmul(py[:], slab[:, j, :], gT[:, j, :],
                                              start=(j == 0), stop=(j == FC - 1)),
                     reads=[d_slab, d_g[j]], writes=[d_py], inc=(j == FC - 1))
            P.op("dve", lambda e: e.tensor_tensor(out=yo[:], in0=py[:], in1=xr[:], op=ALU.add),
                 reads=[d_py, d_xr], writes=[d_yo])
            P.dma("sp", yT[i, :, t * TT:(t + 1) * TT], yo[:], f"s_y{g % 2}", reads=[d_yo], final=True)
    if DBG:
        dh = P.dout("dbg_h", [128, DC, TT], BF16)
        dg = P.dout("dbg_g", [128, FC, TT], BF16)
        P.dma("sp", dh, h[:], "s_dbg", reads=[d_h], final=True)
        P.dma("sp", dg, gT[:], "s_dbg", reads=d_g, final=True)
    return P.finish()


def ffn_layouts(x_tm_halo, gain, cw, w_up, w_dn, DC=16, FC=43):
    D = DC * 128
    F = FC * 128
    xT = np.ascontiguousarray(x_tm_halo.T.reshape(DC, 128, -1))
    g = np.ascontiguousarray(gain.reshape(DC, 128).T)
    c = np.ascontiguousarray(cw.reshape(3, FC, 128).transpose(2, 1, 0).reshape(128, FC * 3))
    wu = w_up.reshape(DC, 128, 2, FC, 128).transpose(3, 1, 0, 2, 4).reshape(FC, 128, DC * 2 * 128)
    wd = w_dn.reshape(FC, 128, DC, 128).transpose(2, 1, 0, 3).reshape(DC, 128, FC * 128)
    return xT, g, c, np.ascontiguousarray(wu), np.ascontiguousarray(wd)


EPS = 1e-6


class SlabStream:
    def __init__(self, P, name, shape, dt, srcs, ring, q="sp"):
        self.P = P
        self.name = name
        self.srcs = srcs
        self.ring = ring
        self.q = q
        self.slots = [(P.sb(f"{name}{i}", shape, dt), Dep()) for i in range(ring)]
        self.issued = 0
        self.flat = "p " + " ".join(f"a{i}" for i in range(len(shape) - 1)) + " -> p (" + \
                    " ".join(f"a{i}" for i in range(len(shape) - 1)) + ")"

    def use(self, n):
        P = self.P
        while self.issued < min(len(self.srcs), n + self.ring):
            m = self.issued
            s, d = self.slots[m % self.ring]
            P.dma(self.q, s[:].rearrange(self.flat), self.srcs[m], f"s_{self.name}{m % self.ring}", writes=[d])
            self.issued += 1
        return self.slots[n % self.ring]


def rms_stream(P, x_src_of, n, h_of, d_h, gain, ones, psR, d_psR, xc, sq_tiles, rstd, d_rstd, DC):
    k = 0
    for c in range(DC):
        xt, d_xt = xc[k % len(xc)]; k += 1
        sq, d_sq = sq_tiles[c % len(sq_tiles)]
        P.dma("sp", xt[:, :n], x_src_of(c), f"s_xc{(k - 1) % len(xc)}", writes=[d_xt])
        P.op("act", lambda e: e.activation(out=sq[:, :n], in_=xt[:, :n], func=AF.Square), reads=[d_xt], writes=[d_sq])
        P.op("pe", lambda e: e.matmul(psR[:, :n], ones[:], sq[:, :n], start=(c == 0), stop=(c == DC - 1)),
             reads=[d_sq], writes=[d_psR])
    P.op("act", lambda e: e.activation(out=rstd[:, :n], in_=psR[:, :n], func=AF.Sqrt, bias=EPS, scale=1.0),
         reads=[d_psR], writes=[d_rstd])
    P.op("dve", lambda e: e.reciprocal(rstd[:, :n], rstd[:, :n]), reads=[d_rstd], writes=[d_rstd])
    for c in range(DC):
        xt, d_xt = xc[k % len(xc)]; k += 1
        P.dma("sp", xt[:, :n], x_src_of(c), f"s_xc{(k - 1) % len(xc)}", writes=[d_xt])
        P.op("dve", lambda e: e.scalar_tensor_tensor(out=h_of(c), in0=xt[:, :n], scalar=gain[:, c:c + 1],
                                                      in1=rstd[:, :n], op0=ALU.mult, op1=ALU.mult),
             reads=[d_xt, d_rstd], writes=[d_h])


def build_att(NT, DC=16, NH=16, TT=512):
    D = DC * 128
    P = Prog()
    ntok = NT * TT
    xT = P.din("xT", [DC, 128, TT + ntok], F32)
    gain_d = P.din("gain", [128, DC], F32)
    qk_d = P.din("qkgain", [128, 2], F32)
    hm_d = P.din("hm", [128, 1], F32)
    bt_d = P.din("bt", [128, NH * 5 * 128], BF16)
    wqkv = P.din("wqkv", [12, 128, DC * 512], BF16)
    wo = P.din("wo", [4, 128, NH * 512], BF16)
    yT = P.dout("yT", [DC, 128, ntok], F32)

    h = P.sb("h", [128, DC, TT], BF16); d_h = Dep()
    qT = P.sb("qT", [128, NH, TT], BF16); d_q = [Dep() for _ in range(NH)]
    kring = [P.sb(f"kr{i}", [128, NH, TT], BF16) for i in range(2)]
    d_k = [[Dep() for _ in range(NH)] for _ in range(2)]
    vring = [P.sb(f"vr{i}", [128, 4, D], BF16) for i in range(2)]
    d_v = [[Dep() for _ in range(4)] for _ in range(2)]
    OT = P.sb("OT", [128, NH, TT], BF16); d_o = [Dep() for _ in range(NH)]
    bt = P.sb("bt_s", [128, NH, 5, 128], BF16); d_bt = Dep()
    gain = P.sb("gain_s", [128, DC], F32); d_gain = Dep()
    qkg = P.sb("qkg", [128, 2], F32); d_qkg = Dep()
    hm = P.sb("hm_s", [128, 1], F32); d_hm = Dep()
    onesD = P.sb("onesD", [128, 128], BF16); d_ones = Dep()
    onesH = P.sb("onesH", [128, 128], BF16)
    ones1 = P.sb("ones1", [128, 128], BF16)
    xc = [(P.sb(f"xc{i}", [128, TT], F32), Dep()) for i in range(3)]
    sq_tiles = [(P.sb(f"sq{i}", [128, TT], BF16), Dep()) for i in range(2)]
    rstd = P.sb("rstd", [128, TT], F32); d_rstd = Dep()
    rs2 = [(P.sb(f"rs2{i}", [128, TT], F32), Dep()) for i in range(2)]
    sb_s = [(P.sb(f"sbs{i}", [128, 5, 128], F32), Dep()) for i in range(2)]
    pT = [(P.sb(f"pT{i}", [128, 5, 128], BF16), Dep()) for i in range(2)]
    rinv = [(P.sb(f"rinv{i}", [128, 128], F32), Dep()) for i in range(2)]
    ys = [(P.sb(f"ys{i}", [128, TT], F32), Dep()) for i in range(2)]
    psA = [(P.ps(f"psA{i}", [128, TT]), Dep()) for i in range(2)]
    psR, d_psR = P.ps("psR", [128, TT]), Dep()
    psN = [(P.ps(f"psN{i}", [128, TT]), Dep()) for i in range(1)]
    psS = [(P.ps(f"psS{i}", [128, 1024]), Dep()) for i in range(1)]
    psO = [(P.ps(f"psO{i}", [128, TT]), Dep()) for i in range(2)]

    P.dma("sp", gain[:], gain_d, "s_c", writes=[d_gain])
    P.dma("sp", qkg[:], qk_d, "s_c", writes=[d_qkg])
    P.dma("sp", hm[:], hm_d, "s_c", writes=[d_hm])
    P.dma("sp", bt[:].rearrange("p a b c -> p (a b c)"), bt_d, "s_c", writes=[d_bt])
    P.op("dve", lambda e: e.memset(onesD[:], 1.0 / D), writes=[d_ones])
    P.op("dve", lambda e: e.memset(onesH[:], 1.0 / 128), writes=[d_ones])
    P.op("dve", lambda e: e.memset(ones1[:], 1.0), writes=[d_ones])
    P.op("dve", lambda e: e.tensor_scalar(qkg[:, 0:1], qkg[:, 0:1], float(128 ** -0.5), None, op0=ALU.mult),
         reads=[d_qkg], writes=[d_qkg])

    srcs = []
    for t in range(NT + 1):
        if t >= 1:
            srcs += [wqkv[s] for s in range(0, 4)]
        srcs += [wqkv[s] for s in range(4, 12)]
        if t >= 1:
            srcs += [wo[s] for s in range(4)]
    slabs = SlabStream(P, "slab", [128, DC, 512], BF16, srcs, ring=2)
    sn = 0
    ga = 0
    gs = 0
    for t in range(NT + 1):
        kr, vr = kring[t % 2], vring[t % 2]
        dk_, dv_ = d_k[t % 2], d_v[t % 2]
        rms_stream(P, lambda c: xT[c, :, t * TT:(t + 1) * TT], TT, lambda c: h[:, c, :], d_h, gain, onesD,
                   psR, d_psR, xc, sq_tiles, rstd, d_rstd, DC)
        for which in ((0, 1) if t >= 1 else (1,)):
            for s4 in range(4):
                slab, d_slab = slabs.use(sn); sn += 1
                for hh in range(4):
                    hd = s4 * 4 + hh
                    pa, d_pa = psA[ga % 2]; ga += 1
                    pn, d_pn = psN[0]
                    sq, d_sq = sq_tiles[ga % 2]
                    r2, d_r2 = rs2[ga % 2]
                    for k in range(DC):
                        P.op("pe", lambda e: e.matmul(pa[:], slab[:, k, hh * 128:(hh + 1) * 128], h[:, k, :],
                                                      start=(k == 0), stop=(k == DC - 1)),
                             reads=[d_slab, d_h], writes=[d_pa], inc=(k == DC - 1))
                    P.op("act", lambda e: e.activation(out=sq[:], in_=pa[:], func=AF.Square), reads=[d_pa], writes=[d_sq])
                    P.op("pe", lambda e: e.matmul(pn[:], onesH[:], sq[:], start=True, stop=True),
                         reads=[d_sq], writes=[d_pn])
                    P.op("act", lambda e: e.activation(out=r2[:], in_=pn[:], func=AF.Sqrt, bias=EPS, scale=1.0),
                         reads=[d_pn], writes=[d_r2])
                    P.op("dve", lambda e: e.reciprocal(r2[:], r2[:]), reads=[d_r2], writes=[d_r2])
                    if which == 0:
                        dst, d_dst = qT[:, hd, :], d_q[hd]
                    else:
                        dst, d_dst = kr[:, hd, :], dk_[hd]
                    P.op("dve", lambda e: e.scalar_tensor_tensor(out=dst, in0=pa[:], scalar=qkg[:, which:which + 1],
                                                                  in1=r2[:], op0=ALU.mult, op1=ALU.mult),
                         reads=[d_pa, d_r2, d_qkg], writes=[d_dst])
        for cb in range(4):
            slab, d_slab = slabs.use(sn); sn += 1
            for tb in range(4):
                pa, d_pa = psA[ga % 2]; ga += 1
                for k in range(DC):
                    P.op("pe", lambda e: e.matmul(pa[:], h[:, k, tb * 128:(tb + 1) * 128], slab[:, k, :],
                                                  start=(k == 0), stop=(k == DC - 1)),
                         reads=[d_slab, d_h], writes=[d_pa], inc=(k == DC - 1))
                P.op("act", lambda e: e.copy(out=vr[:, tb, cb * 512:(cb + 1) * 512], in_=pa[:]),
                     reads=[d_pa], writes=[dv_[tb]])
        if t == 0:
            continue
        for hd in range(NH):
            for qb in range(4):
                ps, d_ps = psS[0]
                po, d_po = psO[gs % 2]
                sbs, d_sbs = sb_s[gs % 2]
                pt, d_pt = pT[gs % 2]
                ri, d_ri = rinv[gs % 2]
                gs += 1
                blks = []
                for r in range(5):
                    kb = qb + r - 4
                    tt, bb = (t, kb) if kb >= 0 else (t - 1, kb + 4)
                    blks.append((tt, bb))
                    col = r * 128 if r < 4 else 512
                    P.op("pe", lambda e: e.matmul(ps[:, col:col + 128], kring[tt % 2][:, hd, bb * 128:(bb + 1) * 128],
                                                  qT[:, hd, qb * 128:(qb + 1) * 128], start=True, stop=True),
                         reads=[d_k[tt % 2][hd], d_q[hd]], writes=[d_ps], inc=(r == 4))
                for r in range(5):
                    tt, bb = blks[r]
                    col = r * 128 if r < 4 else 512
                    if t == 1 and tt == 0:
                        P.op("dve", lambda e: e.scalar_tensor_tensor(out=sbs[:, r, :], in0=ps[:, col:col + 128],
                                                                      scalar=hm[:, 0:1], in1=bt[:, hd, r, :],
                                                                      op0=ALU.add, op1=ALU.add),
                             reads=[d_ps, d_hm, d_bt], writes=[d_sbs])
                    else:
                        P.op("dve", lambda e: e.tensor_tensor(out=sbs[:, r, :], in0=ps[:, col:col + 128],
                                                               in1=bt[:, hd, r, :], op=ALU.add),
                             reads=[d_ps, d_bt], writes=[d_sbs])
                P.op("act", lambda e: e.activation(out=pt[:].rearrange("p a b -> p (a b)"),
                                                   in_=sbs[:].rearrange("p a b -> p (a b)"), func=AF.Exp),
                     reads=[d_sbs], writes=[d_pt])
                for r in range(5):
                    tt, bb = blks[r]
                    P.op("pe", lambda e: e.matmul(po[:, 0:128], vring[tt % 2][:, bb, hd * 128:(hd + 1) * 128], pt[:, r, :],
                                                  start=(r == 0), stop=(r == 4)),
                         reads=[d_v[tt % 2][bb], d_pt], writes=[d_po], inc=False)
                for r in range(5):
                    P.op("pe", lambda e: e.matmul(po[:, 128:256], ones1[:], pt[:, r, :], start=(r == 0), stop=(r == 4)),
                         reads=[d_pt], writes=[d_po], inc=(r == 4))
                P.op("dve", lambda e: e.reciprocal(ri[:], po[:, 128:256]), reads=[d_po], writes=[d_ri])
                P.op("dve", lambda e: e.tensor_tensor(out=OT[:, hd, qb * 128:(qb + 1) * 128], in0=po[:, 0:128], in1=ri[:],
                                                       op=ALU.mult),
                     reads=[d_po, d_ri], writes=[d_o[hd]])
        for s4 in range(4):
            slab, d_slab = slabs.use(sn); sn += 1
            for ic in range(4):
                i = s4 * 4 + ic
                pa, d_pa = psA[ga % 2]; ga += 1
                xr, d_xr = xc[ga % 3]
                yo, d_yo = ys[ga % 2]
                P.dma("sp", xr[:], xT[i, :, t * TT:(t + 1) * TT], f"s_xc{ga % 3}", writes=[d_xr])
                for hd in range(NH):
                    P.op("pe", lambda e: e.matmul(pa[:], slab[:, hd, ic * 128:(ic + 1) * 128], OT[:, hd, :],
                                                  start=(hd == 0), stop=(hd == NH - 1)),
                         reads=[d_slab, d_o[hd]], writes=[d_pa], inc=(hd == NH - 1))
                P.op("dve", lambda e: e.tensor_tensor(out=yo[:], in0=pa[:], in1=xr[:], op=ALU.add),
                     reads=[d_pa, d_xr], writes=[d_yo])
                P.dma("sp", yT[i, :, (t - 1) * TT:t * TT], yo[:], f"s_y{ga % 2}", reads=[d_yo], final=True)
    return P.finish()


def att_bias_table(rel_bias):
    H = rel_bias.shape[0]
    p = np.arange(128)[:, None, None]
    r = np.arange(5)[None, :, None]
    q = np.arange(128)[None, None, :]
    rel = q - p + 128 * (4 - r)
    qc = (q >= 64).astype(int) + 0 * r + 0 * p
    kc = 2 * (r - 4) + (p >= 64).astype(int) + 0 * q
    valid = (qc - kc >= 0) & (qc - kc <= 8)
    idx = np.clip(rel, -63, 256) + 63
    out = rel_bias[:, idx]
    out = np.where(valid[None], out, np.float32(-1e30))
    return np.ascontiguousarray(out.transpose(1, 0, 2, 3)).reshape(128, H * 5 * 128)


def att_layouts(w_qkv, w_o, DC=16, NH=16):
    D = DC * 128
    a = w_qkv.reshape(DC, 128, 12, 512).transpose(2, 1, 0, 3).reshape(12, 128, DC * 512)
    b = w_o.reshape(NH, 128, 4, 512).transpose(2, 1, 0, 3).reshape(4, 128, NH * 512)
    return np.ascontiguousarray(a), np.ascontiguousarray(b)


WIN = (2, 4, 8, 16)


def build_pool(NT, DC=16, TT=512, HL=16):
    D = DC * 128
    P = Prog()
    ntok = NT * TT
    xT = P.din("xT", [DC, 128, HL + ntok], F32)
    gain_d = P.din("gain", [128, DC], F32)
    psc_d = P.din("pscale", [128, DC], F32)
    rc_d = P.din("rc", [128, 4 * TT], F32)
    pw_d = P.din("pw", [128, 4 * 4 * 512], BF16)
    yT = P.dout("yT", [DC, 128, ntok], F32)

    xs = P.sb("xs", [128, DC, TT], F32); d_xs = Dep()
    xh = P.sb("xh", [128, DC, HL], F32); d_xh = Dep()
    hb = P.sb("hb", [128, DC, HL + TT], F32); d_hb = Dep()
    wsA = P.sb("wsA", [128, 4, HL + TT], F32); d_wsA = Dep()
    wsB = P.sb("wsB", [128, 4, HL + TT], F32); d_wsB = Dep()
    pooled = P.sb("pooled", [128, DC, TT], BF16); d_pl = [Dep() for _ in range(4)]
    gain = P.sb("gain_s", [128, DC], F32); d_gain = Dep()
    psc = P.sb("psc_s", [128, DC], F32); d_psc = Dep()
    rc = P.sb("rc_s", [128, 4, TT], F32); d_rc = Dep()
    pw = P.sb("pw_s", [128, 4, 4, 512], BF16); d_pw = Dep()
    ones = P.sb("ones", [128, 128], BF16); d_ones = Dep()
    sq_tiles = [(P.sb(f"sq{i}", [128, TT], BF16), Dep()) for i in range(2)]
    rstd = P.sb("rstd", [128, TT], F32); d_rstd = Dep()
    ys = [(P.sb(f"ys{i}", [128, TT], F32), Dep()) for i in range(2)]
    psY = [(P.ps(f"psY{i}", [128, TT]), Dep()) for i in range(2)]
    psR, d_psR = P.ps("psR", [128, TT]), Dep()

    P.dma("sp", gain[:], gain_d, "s_c", writes=[d_gain])
    P.dma("sp", psc[:], psc_d, "s_c", writes=[d_psc])
    P.dma("sp", rc[:].rearrange("p a b -> p (a b)"), rc_d, "s_c", writes=[d_rc])
    P.dma("sp", pw[:].rearrange("p a b c -> p (a b c)"), pw_d, "s_c", writes=[d_pw])
    P.op("dve", lambda e: e.memset(ones[:], 1.0 / D), writes=[d_ones])
    xTv = xT.rearrange("c p t -> p c t")
    P.dma("sp", xh[:], xTv[:, :, 0:HL], "s_xh", writes=[d_xh])
    rms_cols(P, lambda c: xh[:, c, :], HL, lambda c: hb[:, c, 0:HL], gain, ones, psR, d_psR, sq_tiles,
             rstd, d_rstd, d_xh, d_hb, DC, D)
    gy = 0
    for t in range(NT):
        P.dma("sp", xs[:], xTv[:, :, HL + t * TT: HL + (t + 1) * TT], "s_x", writes=[d_xs])
        if t > 0:
            P.op("pool", lambda e: e.tensor_copy(hb[:, :, 0:HL], hb[:, :, TT:TT + HL]), reads=[d_hb], writes=[d_hb])
        rms_cols(P, lambda c: xs[:, c, :], TT, lambda c: hb[:, c, HL:HL + TT], gain, ones, psR, d_psR, sq_tiles,
                 rstd, d_rstd, d_xs, d_hb, DC, D)
        N = HL + TT
        for g in range(4):
            src, d_src = hb[:, 4 * g:4 * g + 4, :], d_hb
            bufs = [(wsA, d_wsA), (wsB, d_wsB)]
            sh = 1
            for step in range(g + 1):
                dst, d_dst = bufs[step % 2]
                lo = 2 * sh - 1
                P.op("dve", lambda e: e.tensor_tensor(out=dst[:, :, lo:N], in0=src[:, :, lo:N], in1=src[:, :, lo - sh:N - sh],
                                                       op=ALU.add),
                     reads=[d_src], writes=[d_dst])
                src, d_src = dst, d_dst
                sh *= 2
            w = WIN[g]
            if t == 0:
                for c in range(4):
                    P.op("dve", lambda e: e.tensor_tensor(out=src[:, c, HL:N], in0=src[:, c, HL:N], in1=rc[:, g, :], op=ALU.mult),
                         reads=[d_src, d_rc], writes=[d_src])
                sc = 1.0
            else:
                sc = 1.0 / w
            P.op("dve", lambda e: e.scalar_tensor_tensor(out=pooled[:, 4 * g:4 * g + 4, :], in0=src[:, :, HL:N], scalar=sc,
                                                          in1=hb[:, 4 * g:4 * g + 4, HL:N], op0=ALU.mult, op1=ALU.subtract),
                 reads=[d_src, d_hb], writes=[d_pl[g]])
            for ec in range(4):
                i = 4 * g + ec
                py, d_py = psY[gy % 2]
                yo, d_yo = ys[gy % 2]
                gy += 1
                for kc in range(4):
                    P.op("pe", lambda e: e.matmul(py[:], pw[:, g, kc, ec * 128:(ec + 1) * 128], pooled[:, 4 * g + kc, :],
                                                  start=(kc == 0), stop=(kc == 3)),
                         reads=[d_pw, d_pl[g]], writes=[d_py], inc=(kc == 3))
                P.op("dve", lambda e: e.scalar_tensor_tensor(out=yo[:], in0=py[:], scalar=psc[:, i:i + 1], in1=xs[:, i, :],
                                                              op0=ALU.mult, op1=ALU.add),
                     reads=[d_py, d_psc, d_xs], writes=[d_yo])
                P.dma("sp", yT[i, :, t * TT:(t + 1) * TT], yo[:], f"s_y{(gy - 1) % 2}", reads=[d_yo], final=True)
    return P.finish()


def pool_rc(first, TT=512):
    rc = np.empty((4, TT), np.float32)
    for g, w in enumerate(WIN):
        if first:
            rc[g] = 1.0 / np.minimum(np.arange(TT) + 1, w)
        else:
            rc[g] = 1.0 / w
    return np.ascontiguousarray(np.broadcast_to(rc.reshape(1, 4 * TT), (128, 4 * TT)))


def pool_layouts(pool_w):
    return np.ascontiguousarray(pool_w.reshape(4, 4, 128, 512).transpose(2, 0, 1, 3).reshape(128, 4 * 4 * 512))


EPS = 1e-6
BIG = 30000.0


def gdn_consts():
    i = np.arange(128)[:, None]
    j = np.arange(128)[None, :]
    same = (i // 64) == (j // 64)
    c = np.zeros((8, 128, 128), np.float32)
    c[0] = (i == j)
    c[1] = same & (i <= j)
    c[2] = same
    c[3] = (i < 64) & (j >= 0)
    c[4] = (i >= 64) & (j >= 0)
    c[5] = np.where(same & (i > j), 0.0, BIG)
    c[6] = np.where(same & (j >= i), 0.0, -BIG)
    c[7] = same & (i > j)
    return np.ascontiguousarray(c.transpose(1, 0, 2))


def build_gdn1(NT, DC=16, NKH=8, NVH=16, TT=512, lvl=9):
    D = DC * 128
    P = Prog()
    ntok = NT * TT
    NCV = 2 * NKH + NVH
    NSL = (NCV + NVH) // 4
    xT = P.din("xT", [DC, 128, ntok], F32)
    gain_d = P.din("gain", [128, DC], F32)
    cw_d = P.din("cw", [128, NCV * 4], F32)
    win = P.din("win", [NSL, 128, DC * 512], BF16)
    wab_d = P.din("wab", [128, DC * 64], BF16)
    hp_d = P.din("hp", [128, 2 * NVH], F32)
    og_d = P.din("ogain", [128, 1], F32)
    cst_d = P.din("cst", [128, 8 * 128], F32)
    og_out = P.dout("og", [NVH, 128, ntok], BF16)

    h = P.sb("h", [128, DC, TT], BF16); d_h = Dep()
    qT = P.sb("qT", [128, NKH, TT], BF16); d_q = [Dep() for _ in range(NKH)]
    kT = P.sb("kT", [128, NKH, TT], BF16); d_k = [Dep() for _ in range(NKH)]
    vT = P.sb("vT", [128, NVH, TT], BF16); d_v = [Dep() for _ in range(NVH)]
    sgT = P.sb("sgT", [128, NVH, TT], BF16); d_sg = [Dep() for _ in range(NVH)]
    oTs = P.sb("oTs", [128, NVH, TT], BF16); d_oT = [Dep() for _ in range(NVH)]
    S32 = P.sb("S32", [128, NVH, 128], F32); d_S32 = [Dep() for _ in range(NVH)]
    S16 = P.sb("S16", [128, NVH, 128], BF16); d_S16 = [Dep() for _ in range(NVH)]
    carry = P.sb("carry", [128, NCV, 3], F32); d_carry = [Dep() for _ in range(NCV)]
    gain = P.sb("gain_s", [128, DC], F32); d_gain = Dep()
    cw = P.sb("cw_s", [128, NCV * 4], F32); d_cw = Dep()
    wab = P.sb("wab_s", [128, DC, 64], BF16); d_wab = Dep()
    hp = P.sb("hp_s", [128, 2 * NVH], F32); d_hp = Dep()
    nA = P.sb("nA", [128, NVH], F32); d_nA = Dep()
    ogain = P.sb("ogain_s", [128, 1], F32); d_og = Dep()
    cst = P.sb("cst_s", [128, 8, 128], F32); d_cst = Dep()
    cstb_all = P.sb("cstb_all", [128, 8, 128], BF16)
    cstb = cstb_all[:, 0, :]
    onesD = P.sb("onesD", [128, 128], BF16); d_ones = Dep()
    onesH = P.sb("onesH", [128, 128], BF16)
    ones1 = P.sb("ones1", [128, 128], BF16)
    xc = [(P.sb(f"xc{i}", [128, TT], F32), Dep()) for i in range(3)]
    sq_tiles = [(P.sb(f"sq{i}", [128, TT], BF16), Dep()) for i in range(2)]
    rstd = P.sb("rstd", [128, TT], F32); d_rstd = Dep()
    rs2 = [(P.sb(f"rs2{i}", [128, TT], F32), Dep()) for i in range(2)]
    ext = [(P.sb(f"ext{i}", [128, TT + 3], F32), Dep()) for i in range(2)]
    acc = [(P.sb(f"acc{i}", [128, TT], F32), Dep()) for i in range(2)]
    sil = [(P.sb(f"sil{i}", [128, TT], F32), Dep()) for i in range(2)]
    ogs = [(P.sb(f"ogs{i}", [128, TT], BF16), Dep()) for i in range(2)]

    import os
    def hilo(src, d_src, hi, d_hi, lo, d_lo, tmp, d_tmp, eng=os.environ.get("HILO_ENG", "dve")):
        P.op(eng, lambda e: e.tensor_copy(hi, src), reads=[d_src], writes=[d_hi])
        P.op(eng, lambda e: e.tensor_copy(tmp, hi), reads=[d_hi], writes=[d_tmp])
        P.op(eng, lambda e: e.tensor_tensor(out=lo, in0=src, in1=tmp, op=ALU.subtract), reads=[d_src, d_tmp], writes=[d_lo])
    gbtm = [(P.sb(f"gbtm{i}", [128, 2 * NVH], F32), Dep()) for i in range(2)]
    gcs = [(P.sb(f"gcs{i}", [128, 2 * NVH], F32), Dep()) for i in range(2)]
    sm = [(P.sb(f"sm{i}", [128, 4 * NVH], F32), Dep()) for i in range(2)]
    dec = [(P.sb(f"dec{i}", [128, 2 * NVH], F32), Dep()) for i in range(2)]
    kks = [(P.sb(f"kks{i}", [128, 128], F32), Dep()) for i in range(2)]
    gbx = [(P.sb(f"gbx{i}", [128, NVH], BF16 if i < 2 else F32), Dep()) for i in range(4)]
    gcx = [(P.sb(f"gcx{i}", [128, NVH], BF16 if i == 0 else F32), Dep()) for i in range(3)]
    qks = [(P.sb(f"qks{i}", [128, 128], F32), Dep()) for i in range(2)]
    ems = [(P.sb(f"ems{i}", [128, 2 * NVH], F32), Dep()) for i in range(2)]
    scr = P.sb("scr", [128, 8], F32); d_scr = Dep()
    NW = 2

    def wset(i):
        d = {}
        for nm, dt in (("vb", BF16), ("kbg", BF16), ("kst0", BF16), ("kst1", BF16), ("dg", BF16), ("dgl", BF16), ("Dm", F32), ("DTm", F32),
                       ("nS", F32), ("nT", F32),
                       ("X0", BF16), ("X1", BF16), ("Y0", BF16), ("Y1", BF16), ("P0", BF16), ("P1", BF16),
                       ("u", F32), ("wT", BF16), ("attnT", BF16), ("egr", F32), ("qgT", BF16), ("vnew", BF16)):
            d[nm] = (P.sb(f"w{i}_{nm}", [128, 128], dt), Dep())
        return d
    ws = [wset(i) for i in range(NW)]
    for i in range(NW):
        P.op("dve", lambda e: e.memset(ws[i]["vnew"][0][:], 0.0), writes=[ws[i]["vnew"][1]])

    psA = [(P.ps(f"psA{i}", [128, TT]), Dep(True)) for i in range(2)]
    psR, d_psR = P.ps("psR", [128, TT]), Dep(True)
    psN, d_psN = P.ps("psN", [128, TT]), Dep(True)
    psm_t = [P.ps(f"psm{i}", [128, TT]) for i in range(4)]
    ring = [(t_, Dep(True)) for t_ in psm_t]
    pctr = [0]

    def bank():
        r = ring[pctr[0] % len(ring)]
        pctr[0] += 1
        return r

    def sub(t_, k):
        return t_[:, k * 128:(k + 1) * 128]

    P.dma("sp", gain[:], gain_d, "s_c", writes=[d_gain])
    P.dma("sp", cw[:], cw_d, "s_c", writes=[d_cw])
    P.dma("sp", wab[:].rearrange("p a b -> p (a b)"), wab_d, "s_c", writes=[d_wab])
    P.dma("sp", hp[:], hp_d, "s_c", writes=[d_hp])
    P.dma("sp", ogain[:], og_d, "s_c", writes=[d_og])
    P.dma("sp", cst[:].rearrange("p a b -> p (a b)"), cst_d, "s_c", writes=[d_cst])
    P.op("dve", lambda e: e.memset(onesD[:], 1.0 / D), writes=[d_ones])
    P.op("dve", lambda e: e.memset(onesH[:], 1.0 / 128), writes=[d_ones])
    P.op("dve", lambda e: e.memset(ones1[:], 1.0), writes=[d_ones])
    P.op("dve", lambda e: e.tensor_copy(cstb_all[:, 0:5, :], cst[:, 0:5, :]), reads=[d_cst], writes=[d_ones])
    P.op("dve", lambda e: e.tensor_copy(cstb_all[:, 7, :], cst[:, 7, :]), reads=[d_cst], writes=[d_ones])
    P.op("dve", lambda e: e.memset(carry[:].rearrange("p a b -> p (a b)"), 0.0), writes=d_carry)
    P.op("dve", lambda e: e.memset(S32[:].rearrange("p a b -> p (a b)"), 0.0), writes=d_S32)
    P.op("dve", lambda e: e.memset(S16[:].rearrange("p a b -> p (a b)"), 0.0), writes=d_S16)
    P.op("act", lambda e: e.activation(out=nA[:], in_=hp[:, 0:NVH], func=AF.Exp), reads=[d_hp], writes=[d_nA])
    P.op("dve", lambda e: e.tensor_scalar(nA[:], nA[:], -1.0, None, op0=ALU.mult), reads=[d_nA], writes=[d_nA])
    ident, tri, same, ind0, ind1, maskS, maskT = [cst[:, i, :] for i in range(7)]
    P.sync_all([d_gain, d_cw, d_wab, d_hp, d_og, d_cst, d_ones, d_nA] + d_carry + d_S32 + d_S16)

    PROBE = int(os.environ.get("PROBE", "-1"))
    pctr2 = [0]

    def probe(tag):
        if PROBE != tag:
            return
        for sname in os.environ.get("PSEM", "e_dve,e_act").split(","):
            P.wait("pe", (sname, P.cnt[sname] - int(os.environ.get("PLAG", "0"))))
        if os.environ.get("PT") == "A":
            P.op("pe", lambda e: e.matmul(psA[0][0][:, 0:128], cstb, cstb, start=True, stop=True), writes=[psA[0][1]])
        elif os.environ.get("PT") == "none":
            pass
        else:
            P.op("pe", lambda e: e.matmul(psN[:, 0:128], cstb, cstb, start=True, stop=True), writes=[d_psN])

    slabs = SlabStream(P, "slab", [128, DC, 512], BF16, [win[s] for _ in range(NT) for s in range(NSL)], ring=2)
    sn = 0
    ga = 0
    wi = 0
    for t in range(NT):
        rms_stream(P, lambda c: xT[c, :, t * TT:(t + 1) * TT], TT, lambda c: h[:, c, :], d_h, gain, onesD,
                   psR, d_psR, xc, sq_tiles, rstd, d_rstd, DC)
        for s in range(NSL if not os.environ.get("NOSTAGEA") else 0):
            slab, d_slab = slabs.use(sn); sn += 1
            for hh in range(4):
                hs = s * 4 + hh
                pa, d_pa = psA[ga % 2]
                ex, d_ex = ext[ga % 2]
                ac, d_ac = acc[ga % 2]
                si, d_si = sil[ga % 2]
                sq, d_sq = sq_tiles[ga % 2]
                r2, d_r2 = rs2[ga % 2]
                ga += 1
                for k in range(DC):
                    P.op("pe", lambda e: e.matmul(pa[:], slab[:, k, hh * 128:(hh + 1) * 128], h[:, k, :],
                                                  start=(k == 0), stop=(k == DC - 1)),
                         reads=[d_slab, d_h], writes=[d_pa], inc=(k == DC - 1))
                if hs >= NCV:
                    vh = hs - NCV
                    P.op("act", lambda e: e.activation(out=sgT[:, vh, :], in_=pa[:], func=AF.Silu),
                         reads=[d_pa], writes=[d_sg[vh]])
                    continue
                P.op("pool", lambda e: e.tensor_copy(ex[:, 0:3], carry[:, hs, :]), reads=[d_carry[hs]], writes=[d_ex])
                P.op("act", lambda e: e.copy(out=ex[:, 3:TT + 3], in_=pa[:]), reads=[d_pa], writes=[d_ex])
                P.op("pool", lambda e: e.tensor_copy(carry[:, hs, :], ex[:, TT:TT + 3]), reads=[d_ex], writes=[d_carry[hs]])
                P.op("dve", lambda e: e.tensor_scalar(ac[:], ex[:, 0:TT], cw[:, 4 * hs:4 * hs + 1], None, op0=ALU.mult),
                     reads=[d_ex, d_cw], writes=[d_ac])
                for j in range(1, 4):
                    P.op("dve", lambda e: e.scalar_tensor_tensor(out=ac[:], in0=ex[:, j:TT + j],
                                                                  scalar=cw[:, 4 * hs + j:4 * hs + j + 1],
                                                                  in1=ac[:], op0=ALU.mult, op1=ALU.add),
                         reads=[d_ex, d_ac], writes=[d_ac])
                if hs >= 2 * NKH:
                    vh = hs - 2 * NKH
                    P.op("act", lambda e: e.activation(out=vT[:, vh, :], in_=ac[:], func=AF.Silu),
                         reads=[d_ac], writes=[d_v[vh]])
                    continue
                P.op("act", lambda e: e.activation(out=si[:], in_=ac[:], func=AF.Silu), reads=[d_ac], writes=[d_si])
                P.op("act", lambda e: e.activation(out=sq[:], in_=si[:], func=AF.Square), reads=[d_si], writes=[d_sq])
                P.op("pe", lambda e: e.matmul(psN[:], ones1[:], sq[:], start=True, stop=True), reads=[d_sq], writes=[d_psN])
                P.op("act", lambda e: e.activation(out=r2[:], in_=psN[:], func=AF.Sqrt, bias=EPS, scale=1.0),
                     reads=[d_psN], writes=[d_r2])
                P.op("dve", lambda e: e.reciprocal(r2[:], r2[:]), reads=[d_r2], writes=[d_r2])
                if hs < NKH:
                    P.op("dve", lambda e: e.scalar_tensor_tensor(out=qT[:, hs, :], in0=si[:], scalar=float(128 ** -0.5),
                                                                  in1=r2[:], op0=ALU.mult, op1=ALU.mult),
                         reads=[d_si, d_r2], writes=[d_q[hs]])
                else:
                    P.op("dve", lambda e: e.tensor_tensor(out=kT[:, hs - NKH, :], in0=si[:], in1=r2[:], op=ALU.mult),
                         reads=[d_si, d_r2], writes=[d_k[hs - NKH]])
        for blk in range(int(os.environ.get("NBLK", "4")) if lvl >= 2 else 0):
            bs = slice(blk * 128, (blk + 1) * 128)
            gb, d_gb = gbtm[blk % 2]
            gc, d_gc = gcs[blk % 2]
            s4, d_s4 = sm[blk % 2]
            dc, d_dc = dec[blk % 2]
            bt_, d_pt = bank(); pt = sub(bt_, 0)
            for k in range(DC):
                P.op("pe", lambda e: e.matmul(pt[:, 0:64], h[:, k, bs], wab[:, k, :], start=(k == 0), stop=(k == DC - 1)),
                     reads=[d_wab, d_h], writes=[d_pt], inc=(k == DC - 1))
            probe(0)
            zt, d_zt = gbx[2]
            P.op("dve", lambda e: e.tensor_tensor(out=zt[:], in0=pt[:, 0:NVH], in1=hp[:, NVH:2 * NVH], op=ALU.add),
                 reads=[d_pt, d_hp], writes=[d_zt])
            P.op("act", lambda e: e.activation(out=zt[:], in_=zt[:], func=AF.Exp), reads=[d_zt], writes=[d_zt])
            P.op("act", lambda e: e.activation(out=zt[:], in_=zt[:], func=AF.Ln, bias=1.0, scale=1.0), reads=[d_zt], writes=[d_zt])
            P.op("dve", lambda e: e.tensor_tensor(out=gb[:, 0:NVH], in0=zt[:], in1=nA[:], op=ALU.mult),
                 reads=[d_zt, d_nA], writes=[d_gb])
            P.op("act", lambda e: e.activation(out=gb[:, NVH:2 * NVH], in_=pt[:, 32:32 + NVH], func=AF.Exp, scale=-1.0),
                 reads=[d_pt], writes=[d_gb])
            P.op("dve", lambda e: e.tensor_scalar(gb[:, NVH:2 * NVH], gb[:, NVH:2 * NVH], 1.0, None, op0=ALU.add),
                 reads=[d_gb], writes=[d_gb])
            P.op("dve", lambda e: e.reciprocal(gb[:, NVH:2 * NVH], gb[:, NVH:2 * NVH]), reads=[d_gb], writes=[d_gb])
            if lvl < 2.2:
                continue
            probe(1)
            bc_, d_pc = bank(); pc = sub(bc_, 0)
            gH, d_gH = gbx[0]; gL, d_gL = gbx[1]; gF, d_gF = gbx[3]
            probe(2)
            if not os.environ.get("NOHILO"):
                hilo(gb[:, 0:NVH], d_gb, gH[:, 0:NVH], d_gH, gL[:, 0:NVH], d_gL, gF[:, 0:NVH], d_gF)
            for ci, cmat in enumerate((1, 7, 3, 4)):
                for part, (src, d_src) in enumerate(((gH, d_gH), (gL, d_gL))):
                    P.op("pe", lambda e: e.matmul(pc[:, ci * NVH:(ci + 1) * NVH], cstb_all[:, cmat, :], (src[:, 0:NVH] if not os.environ.get("ALT") else cstb_all[:, 0, 0:NVH]),
                                                  start=(part == 0), stop=(part == 1)),
                         reads=[d_src, d_ones], writes=[d_pc], inc=(ci == 3 and part == 1))
            probe(3)
            if lvl < 2.3:
                continue
            P.op("act", lambda e: e.copy(out=gc[:], in_=pc[:, 0:2 * NVH]), reads=[d_pc], writes=[d_gc])
            P.op("act", lambda e: e.activation(out=dc[:], in_=pc[:, 2 * NVH:4 * NVH], func=AF.Exp), reads=[d_pc], writes=[d_dc])
            gcHb, d_gcHb = gcx[0]; gcHf, d_gcHf = gcx[1]; gcLf, d_gcLf = gcx[2]
            P.op("dve", lambda e: e.tensor_copy(gcHb[:], gc[:, 0:NVH]), reads=[d_gc], writes=[d_gcHb])
            P.op("dve", lambda e: e.tensor_copy(gcHf[:], gcHb[:]), reads=[d_gcHb], writes=[d_gcHf])
            P.op("dve", lambda e: e.tensor_tensor(out=gcLf[:], in0=gc[:, 0:NVH], in1=gcHf[:], op=ALU.subtract), reads=[d_gc, d_gcHf], writes=[d_gcLf])
            P.op("act", lambda e: e.activation(out=s4[:, 0:2 * NVH], in_=pc[:, 0:2 * NVH], func=AF.Exp), reads=[d_pc], writes=[d_s4])
            ek_t, d_ek = s4, d_s4
            ek = s4[:, NVH:2 * NVH]
            probe(4)
            if lvl < 2.4:
                continue
            probe(42)
            P.op("dve", lambda e: e.tensor_tensor(out=s4[:, 2 * NVH:3 * NVH], in0=s4[:, 0:NVH], in1=gb[:, NVH:2 * NVH], op=ALU.mult),
                 reads=[d_s4, d_gb], writes=[d_s4])
            probe(43)
            P.op("dve", lambda e: e.tensor_scalar(s4[:, 3 * NVH:4 * NVH], gb[:, NVH:2 * NVH], -1.0, None, op0=ALU.mult),
                 reads=[d_gb], writes=[d_s4])
            probe(5)
            em, d_em = ems[blk % 2]
            P.op("dve", lambda e: e.tensor_scalar(em[:, 0:NVH], ek, cst[:, 3, 0:1], None, op0=ALU.mult),
                 reads=[d_ek], writes=[d_em])
            P.op("dve", lambda e: e.tensor_scalar(em[:, NVH:2 * NVH], ek, cst[:, 4, 0:1], None, op0=ALU.mult),
                 reads=[d_ek], writes=[d_em])
            probe(6)
            for kh in range(NKH if lvl >= 3 else 0):
                kk_s, d_kk = kks[kh % 2]
                qk_s, d_qk = qks[kh % 2]
                bk_, d_bk = bank()
                pK, d_pK = sub(bk_, 0), d_bk
                P.op("pe", lambda e: e.matmul(pK, kT[:, kh, bs], cstb, start=True, stop=True),
                     reads=[d_k[kh], d_ones], writes=[d_pK])
                b1_, d_p1 = bank(); p1 = sub(b1_, 0)
                P.op("pe", lambda e: e.matmul(p1, kT[:, kh, bs], kT[:, kh, bs], start=True, stop=True),
                     reads=[d_k[kh]], writes=[d_p1])
                P.op("act", lambda e: e.copy(out=kk_s[:], in_=p1), reads=[d_p1], writes=[d_kk])
                b2_, d_p2 = bank(); p2 = sub(b2_, 0)
                P.op("pe", lambda e: e.matmul(p2, kT[:, kh, bs], qT[:, kh, bs], start=True, stop=True),
                     reads=[d_k[kh], d_q[kh]], writes=[d_p2])
                P.op("act", lambda e: e.copy(out=qk_s[:], in_=p2), reads=[d_p2], writes=[d_qk])
                probe(7)
                Wv = []
                for vv in range(NVH // NKH):
                    hv = kh * (NVH // NKH) + vv
                    W = ws[wi % NW]; wi += 1
                    Wv.append(W)
                    kbg, d_kbg = W["kbg"]
                    P.op("dve", lambda e: e.tensor_scalar(kbg[:], pK, s4[:, 2 * NVH + hv:2 * NVH + hv + 1], None, op0=ALU.mult),
                         reads=[d_pK, d_s4], writes=[d_kbg])
                    for c in range(2):
                        kstc, d_kstc = W[f"kst{c}"]
                        P.op("dve",
                             (lambda e: e.tensor_scalar(kstc[:], pK, em[:, c * NVH + hv:c * NVH + hv + 1], None, op0=ALU.mult)) if True else
                             (lambda e: e.activation(out=kstc[:], in_=pK, func=AF.Copy, scale=em[:, c * NVH + hv:c * NVH + hv + 1])),
                             reads=[d_pK, d_em], writes=[d_kstc])
                probe(8)
                for vv in range(NVH // NKH if lvl >= 3.1 else 0):
                    hv = kh * (NVH // NKH) + vv
                    W = Wv[vv]
                    kbg, d_kbg = W["kbg"]
                    col = lambda base: slice(base + hv, base + hv + 1)
                    bv_, d_bv = bank()
                    pV, d_pV = sub(bv_, 0), d_bv
                    for _rep in range(int(os.environ.get("REP", "1"))):
                        P.op("pe", lambda e: e.matmul((pV if _rep == 0 else sub(bv_, 1)), vT[:, hv, bs], cstb, start=True, stop=True),
                             reads=[d_v[hv], d_ones], writes=[d_pV])
                    vb, d_vb = W["vb"]
                    if os.environ.get("VBACT"):
                        P.op("act", lambda e: e.mul(vb[:], pV, gb[:, col(NVH)]), reads=[d_pV, d_gb], writes=[d_vb])
                    else:
                        P.op("dve", lambda e: e.tensor_scalar(vb[:], pV, gb[:, col(NVH)], None, op0=ALU.mult),
                             reads=[d_pV, d_gb], writes=[d_vb])
                    if os.environ.get("X1"):
                        if os.environ.get("GUARD"):
                            for _g in range(int(os.environ["GUARD"])):
                                P.op("dve", lambda e: e.memset(acc[0][0][:], 0.0), reads=[d_pV], writes=[acc[0][1]])
                        if os.environ.get("X2") == "3":
                            P.op("act", lambda e: e.copy(out=scr[:, 0:8], in_=vb[:, 0:8]), reads=[d_vb], writes=[d_scr])
                            P.op("pe", lambda e: e.matmul(psN[:, 0:128], vT[:, hv, bs], cstb, start=True, stop=True),
                                 reads=[d_v[hv], d_ones, d_scr], writes=[d_psN])
                        elif os.environ.get("X2"):
                            P.op("pe", lambda e: e.matmul(psN[:, 0:128], vT[:, hv, bs], cstb, start=True, stop=True),
                                 reads=[d_v[hv], d_ones] + ([d_vb] if os.environ["X2"] == "2" else []), writes=[d_psN])
                        else:
                            P.op("pe", lambda e: e.matmul(sub(bv_, int(os.environ["X1"])), vT[:, hv, bs], cstb, start=True, stop=True),
                                 reads=[d_v[hv], d_ones, d_vb], writes=[d_pV])
                    if lvl < 3.4:
                        continue
                    dg, d_dg = W["dg"]
                    dgl, d_dgl = W["dgl"]
                    P.op("dve", lambda e: e.tensor_scalar(dg[:], ident, gcHf[:, hv:hv + 1], None, op0=ALU.mult),
                         reads=[d_gcHf], writes=[d_dg])
                    P.op("dve", lambda e: e.tensor_scalar(dgl[:], ident, gcLf[:, hv:hv + 1], None, op0=ALU.mult),
                         reads=[d_gcLf], writes=[d_dgl])
                    if lvl < 3.5:
                        continue
                    bg_, d_pg = bank(); pg = sub(bg_, 0)
                    if os.environ.get("V1"):
                        P.op("pe", lambda e: e.matmul(pg, (vT[:, hv, bs] if os.environ.get("V5") else cstb_all[:, 2, :]), (cstb if os.environ.get("V3") else dg[:]), start=True, stop=True),
                             reads=([] if os.environ.get("V4") else [d_dg]), writes=[d_pg])
                    else:
                        P.op("pe", lambda e: e.matmul(pg, ones1[:], dg[:], start=True, stop=False),
                             reads=[d_dg], writes=[d_pg], inc=False)
                        P.op("pe", lambda e: e.matmul(pg, ones1[:], dgl[:], start=False, stop=True),
                             reads=[d_dgl], writes=[d_pg])
                    if lvl < 3.6:
                        continue
                    nS, d_nS = W["nS"]
                    nT_, d_nT = W["nT"]
                    Dm, d_Dm = W["Dm"]
                    DTm, d_DT = W["DTm"]
                    egr, d_egr = W["egr"]
                    P.op("dve", lambda e: e.scalar_tensor_tensor(out=nS[:], in0=pg, scalar=gc[:, col(0)], in1=maskS,
                                                                  op0=ALU.subtract, op1=ALU.add),
                         reads=[d_pg, d_gc, d_cst], writes=[d_nS])
                    P.op("act", lambda e: e.activation(out=Dm[:], in_=nS[:], func=AF.Exp, scale=-1.0),
                         reads=[d_nS], writes=[d_Dm])
                    P.op("dve", lambda e: e.scalar_tensor_tensor(out=nT_[:], in0=pg, scalar=gc[:, col(0)], in1=maskT,
                                                                  op0=ALU.subtract, op1=ALU.add),
                         reads=[d_pg, d_gc, d_cst], writes=[d_nT])
                    P.op("act", lambda e: e.activation(out=DTm[:], in_=nT_[:], func=AF.Exp), reads=[d_nT], writes=[d_DT])
                    P.op("act", lambda e: e.activation(out=egr[:], in_=pg, func=AF.Exp), reads=[d_pg], writes=[d_egr])
                    if lvl < 3.8:
                        continue
                    X, d_X = W["X0"]
                    P.op("dve", lambda e: e.scalar_tensor_tensor(out=X[:], in0=kk_s[:], scalar=s4[:, col(3 * NVH)], in1=Dm[:],
                                                                  op0=ALU.mult, op1=ALU.mult),
                         reads=[d_kk, d_s4, d_Dm], writes=[d_X])
                    attnT, d_at = W["attnT"]
                    P.op("dve", lambda e: e.tensor_tensor(out=attnT[:], in0=qk_s[:], in1=DTm[:], op=ALU.mult),
                         reads=[d_qk, d_DT], writes=[d_at])
                    qgT, d_qg = W["qgT"]
                    P.op("dve", lambda e: e.tensor_tensor(out=qgT[:], in0=qT[:, kh, bs], in1=egr[:], op=ALU.mult),
                         reads=[d_q[kh], d_egr], writes=[d_qg])
                    if lvl < 5:
                        continue
                    Y, d_Y = W["Y0"]
                    Pm, d_P = W["P0"]
                    by_, d_py = bank(); py = sub(by_, 0)
                    P.op("pe", lambda e: e.matmul(py, X[:], cstb, start=True, stop=True), reads=[d_X, d_ones], writes=[d_py])
                    P.op("act", lambda e: e.copy(out=Y[:], in_=py), reads=[d_py], writes=[d_Y])
                    P.op("dve", lambda e: e.tensor_tensor(out=Pm[:], in0=Y[:], in1=cstb, op=ALU.add),
                         reads=[d_Y, d_ones], writes=[d_P])
                    for kstep in range(1, 1 + int(os.environ.get("KSTEPS", "5"))):
                        Xn, d_Xn = W[f"X{kstep % 2}"]
                        Yn, d_Yn = W[f"Y{kstep % 2}"]
                        Pn, d_Pn = W[f"P{kstep % 2}"]
                        bx_, d_px = bank(); px = sub(bx_, 0)
                        P.op("pe", lambda e: e.matmul(px, Y[:], X[:], start=True, stop=True), reads=[d_Y, d_X], writes=[d_px])
                        if kstep < 5:
                            pyy, d_pyy = sub(bx_, 1), d_px
                            P.op("pe", lambda e: e.matmul(pyy, X[:], Y[:], start=True, stop=True),
                                 reads=[d_Y, d_X], writes=[d_pyy])
                        P.op("act", lambda e: e.copy(out=Xn[:], in_=px), reads=[d_px], writes=[d_Xn])
                        if kstep < 5:
                            P.op("dve", lambda e: e.tensor_copy(Yn[:], pyy), reads=[d_pyy], writes=[d_Yn])
                        bp_, d_pp = bank(); pp = sub(bp_, 0)
                        P.op("pe", lambda e: e.matmul(pp, Xn[:], Pm[:], start=True, stop=True), reads=[d_Xn, d_P], writes=[d_pp])
                        P.op("dve", lambda e: e.tensor_tensor(out=Pn[:], in0=pp, in1=Pm[:], op=ALU.add),
                             reads=[d_pp, d_P], writes=[d_Pn])
                        X, d_X, Y, d_Y, Pm, d_P = Xn, d_Xn, Yn, d_Yn, Pn, d_Pn
                    TTm, d_TT = Pm, d_P
                    if lvl < 6:
                        continue
                    u, d_u = W["u"]
                    wT, d_wT = W["wT"]
                    bu_, d_pu = bank(); pu = sub(bu_, 0)
                    P.op("pe", lambda e: e.matmul(pu, TTm[:], vb[:], start=True, stop=True), reads=[d_TT, d_vb], writes=[d_pu])
                    P.op("act", lambda e: e.copy(out=u[:], in_=pu), reads=[d_pu], writes=[d_u])
                    pw_, d_pw = sub(bu_, 1), d_pu
                    P.op("pe", lambda e: e.matmul(pw_, kbg[:], TTm[:], start=True, stop=True), reads=[d_TT, d_kbg], writes=[d_pw])
                    P.op("act", lambda e: e.copy(out=wT[:], in_=pw_), reads=[d_pw], writes=[d_wT])
                    vnew, d_vn = W["vnew"]
                    po_t, d_po = psA[hv % 2]; po = sub(po_t, 0)
                    for c in range(2):
                        rs = slice(64 * c, 64 * c + 64)
                        kstc, d_kstc = W[f"kst{c}"]
                        bw_, d_pws = bank(); pws = sub(bw_, 0)
                        P.op("pe", lambda e: e.matmul(pws, wT[:], S16[:, hv, :], start=True, stop=True),
                             reads=[d_wT, d_S16[hv]], writes=[d_pws])
                        P.op("dve", lambda e: e.tensor_tensor(out=vnew[rs, :], in0=u[rs, :], in1=pws[rs, :], op=ALU.subtract),
                             reads=[d_u, d_pws], writes=[d_vn])
                        P.op("pe", lambda e: e.matmul(po[:, rs], S16[:, hv, :], qgT[:, rs], start=True, stop=False),
                             reads=[d_S16[hv], d_qg], writes=[d_po], inc=False)
                        P.op("pe", lambda e: e.matmul(po[:, rs], vnew[:], attnT[:, rs], start=False, stop=True),
                             reads=[d_vn, d_at], writes=[d_po])
                        bS_, d_pS = bank(); pS = sub(bS_, 0)
                        P.op("pe", lambda e: e.matmul(pS, kstc[:], vnew[:], start=True, stop=True),
                             reads=[d_kstc, d_vn], writes=[d_pS])
                        P.op("dve", lambda e: e.scalar_tensor_tensor(out=S32[:, hv, :], in0=S32[:, hv, :],
                                                                      scalar=dc[:, c * NVH + hv:c * NVH + hv + 1], in1=pS,
                                                                      op0=ALU.mult, op1=ALU.add),
                             reads=[d_S32[hv], d_dc, d_pS], writes=[d_S32[hv]])
                        P.op("act", lambda e: e.copy(out=S16[:, hv, :], in_=S32[:, hv, :]), reads=[d_S32[hv]], writes=[d_S16[hv]])
                    P.op("act", lambda e: e.copy(out=oTs[:, hv, bs], in_=po), reads=[d_po], writes=[d_oT[hv]])
        for hv in range(NVH if lvl >= 6 else 0):
            sq, d_sq = sq_tiles[hv % 2]
            r2, d_r2 = rs2[hv % 2]
            og, d_ogs = ogs[hv % 2]
            P.op("act", lambda e: e.activation(out=sq[:], in_=oTs[:, hv, :], func=AF.Square), reads=[d_oT[hv]], writes=[d_sq])
            P.op("pe", lambda e: e.matmul(psN[:], onesH[:], sq[:], start=True, stop=True), reads=[d_sq], writes=[d_psN])
            P.op("act", lambda e: e.activation(out=r2[:], in_=psN[:], func=AF.Sqrt, bias=EPS, scale=1.0),
                 reads=[d_psN], writes=[d_r2])
            P.op("dve", lambda e: e.reciprocal(r2[:], r2[:]), reads=[d_r2], writes=[d_r2])
            P.op("dve", lambda e: e.scalar_tensor_tensor(out=r2[:], in0=oTs[:, hv, :], scalar=ogain[:, 0:1], in1=r2[:],
                                                          op0=ALU.mult, op1=ALU.mult),
                 reads=[d_oT[hv], d_r2, d_og], writes=[d_r2])
            P.op("dve", lambda e: e.tensor_tensor(out=og[:], in0=r2[:], in1=sgT[:, hv, :], op=ALU.mult),
                 reads=[d_r2, d_sg[hv]], writes=[d_ogs])
            P.dma("sp", og_out[hv, :, t * TT:(t + 1) * TT], og[:], f"s_og{hv % 2}", reads=[d_ogs], final=True)
    if lvl < 6:
        dbg = P.dout("dbg", [128, 160], F32)
        P.dma("sp", dbg[:, 0:32], gbtm[1][0][:], "s_dbg", reads=[gbtm[1][1]], final=True)
        P.dma("sp", dbg[:, 32:64], gcs[1][0][:], "s_dbg", reads=[gcs[1][1]], final=True)
        P.dma("sp", dbg[:, 64:128], sm[1][0][:], "s_dbg", reads=[sm[1][1]], final=True)
        P.dma("sp", dbg[:, 128:160], dec[1][0][:], "s_dbg", reads=[dec[1][1]], final=True)
    return P.finish()


def gdn1_layouts(w_in, conv_w, a_log, dt_bias, r, NKH=8, NVH=16, DC=16):
    KEY, VAL = 2048, 4096
    kq = slice(NKH * 128 * r, NKH * 128 * (r + 1))
    vv = slice(NVH * 128 * r, NVH * 128 * (r + 1))
    q_cols = w_in[:, 0:KEY][:, kq]
    k_cols = w_in[:, KEY:2 * KEY][:, kq]
    v_cols = w_in[:, 2 * KEY:2 * KEY + VAL][:, vv]
    g_cols = w_in[:, 2 * KEY + VAL:2 * KEY + 2 * VAL][:, vv]
    base = 2 * KEY + 2 * VAL
    a_cols = w_in[:, base:base + 32][:, NVH * r:NVH * (r + 1)]
    b_cols = w_in[:, base + 32:base + 64][:, NVH * r:NVH * (r + 1)]
    big = np.concatenate([q_cols, k_cols, v_cols, g_cols], 1)
    nsl = big.shape[1] // 512
    win = big.reshape(DC, 128, nsl, 512).transpose(2, 1, 0, 3).reshape(nsl, 128, DC * 512)
    ab = np.zeros((w_in.shape[0], 64), np.float32)
    ab[:, 0:NVH] = a_cols
    ab[:, 32:32 + NVH] = b_cols
    wab = ab.reshape(DC, 128, 64).transpose(1, 0, 2).reshape(128, DC * 64)
    cq = conv_w[:, 0:KEY][:, kq]; ck = conv_w[:, KEY:2 * KEY][:, kq]; cv = conv_w[:, 2 * KEY:][:, vv]
    cc = np.concatenate([cq, ck, cv], 1)
    ncv = cc.shape[1] // 128
    cw = cc.reshape(4, ncv, 128).transpose(2, 1, 0).reshape(128, ncv * 4)
    hp = np.concatenate([a_log[NVH * r:NVH * (r + 1)], dt_bias[NVH * r:NVH * (r + 1)]]).astype(np.float32)
    hp = np.ascontiguousarray(np.broadcast_to(hp[None, :], (128, 2 * NVH)))
    return np.ascontiguousarray(win), np.ascontiguousarray(wab), np.ascontiguousarray(cw), hp


def build_gdn2(NT, DC=16, NH=32, TT=512):
    P = Prog()
    ntok = NT * TT
    xT = P.din("xT", [DC, 128, ntok], F32)
    ogT = P.din("ogT", [NH, 128, ntok], BF16)
    wo = P.din("wo", [4, 128, NH * 512], BF16)
    yT = P.dout("yT", [DC, 128, ntok], F32)
    ogt = [(P.sb(f"ogt{i}", [128, NH, TT], BF16), Dep()) for i in range(2)]
    xc = [(P.sb(f"xc{i}", [128, TT], F32), Dep()) for i in range(3)]
    ys = [(P.sb(f"ys{i}", [128, TT], F32), Dep()) for i in range(2)]
    psA = [(P.ps(f"psA{i}", [128, TT]), Dep(True)) for i in range(2)]
    slabs = SlabStream(P, "slab", [128, NH, 512], BF16, [wo[s] for _ in range(NT) for s in range(4)], ring=2)
    ogv = ogT.rearrange("h p t -> p h t")
    sn = 0
    ga = 0
    for t in range(NT):
        og, d_og = ogt[t % 2]
        P.dma("sp", og[:], ogv[:, :, t * TT:(t + 1) * TT], f"s_og{t % 2}", writes=[d_og])
        for s4 in range(4):
            slab, d_slab = slabs.use(sn); sn += 1
            for ic in range(4):
                i = s4 * 4 + ic
                pa, d_pa = psA[ga % 2]; ga += 1
                xr, d_xr = xc[ga % 3]
                yo, d_yo = ys[ga % 2]
                P.dma("sp", xr[:], xT[i, :, t * TT:(t + 1) * TT], f"s_xc{ga % 3}", writes=[d_xr])
                for hd in range(NH):
                    P.op("pe", lambda e: e.matmul(pa[:], slab[:, hd, ic * 128:(ic + 1) * 128], og[:, hd, :],
                                                  start=(hd == 0), stop=(hd == NH - 1)),
                         reads=[d_slab, d_og], writes=[d_pa], inc=(hd == NH - 1))
                P.op("dve", lambda e: e.tensor_tensor(out=yo[:], in0=pa[:], in1=xr[:], op=ALU.add),
                     reads=[d_pa, d_xr], writes=[d_yo])
                P.dma("sp", yT[i, :, t * TT:(t + 1) * TT], yo[:], f"s_y{ga % 2}", reads=[d_yo], final=True)
    return P.finish()


CW = 2048


def build_cast(R):
    P = Prog()
    src = P.din("w32", [R, 128, CW], F32)
    dst = P.dout("w16", [R, 128, CW], BF16)
    a = [(P.sb(f"a{i}", [128, CW], F32), Dep()) for i in range(3)]
    b = [(P.sb(f"b{i}", [128, CW], BF16), Dep()) for i in range(3)]
    for r in range(R):
        at, d_a = a[r % 3]
        bt, d_b = b[r % 3]
        P.dma("sp", at[:], src[r], f"s_a{r % 3}", writes=[d_a])
        if r % 2 == 0:
            P.op("dve", lambda e: e.tensor_copy(bt[:], at[:]), reads=[d_a], writes=[d_b])
        else:
            P.op("act", lambda e: e.copy(out=bt[:], in_=at[:]), reads=[d_a], writes=[d_b])
        P.dma("sp", dst[r], bt[:], f"s_b{r % 3}", reads=[d_b], final=True)
    return P.finish()


NCORE = 8
SEQ = 8192
HALF = 4096


def _run(nc, in_maps):
    res = run_bass_kernel_spmd(nc, in_maps, core_ids=list(range(NCORE)))
    return res.results


def _cast_all(arrs):
    flat = []
    for a_ in arrs:
        a3 = a_ if a_.ndim == 3 else a_[None]
        flat.append(np.ascontiguousarray(a3.transpose(1, 0, 2)).reshape(128, -1))
    cols = [f.shape[1] for f in flat]
    big = np.concatenate(flat, axis=1)
    C = big.shape[1]
    per = -(-C // (NCORE * CW)) * CW
    pad = per * NCORE - C
    if pad:
        big = np.concatenate([big, np.zeros((128, pad), np.float32)], axis=1)
    R = per // CW
    nc = build_cast(R)
    in_maps = []
    for c in range(NCORE):
        part = big[:, c * per:(c + 1) * per].reshape(128, R, CW).transpose(1, 0, 2)
        in_maps.append({"w32": np.ascontiguousarray(part)})
    res = _run(nc, in_maps)
    outs = [np.asarray(res[c]["w16"]).transpose(1, 0, 2).reshape(128, per) for c in range(NCORE)]
    big16 = np.concatenate(outs, axis=1)[:, :C]
    out = []
    o = 0
    for a_, n in zip(arrs, cols):
        a3shape = a_.shape if a_.ndim == 3 else (1,) + a_.shape
        piece = big16[:, o:o + n].reshape(128, a3shape[0], a3shape[2]).transpose(1, 0, 2)
        out.append(np.ascontiguousarray(piece).reshape(a_.shape))
        o += n
    return out


def _fm(x_tm):
    return np.ascontiguousarray(x_tm.T).reshape(x_tm.shape[1] // 128, 128, x_tm.shape[0])


def _core_slices(x, halo):
    out = []
    for c in range(NCORE):
        b, hf = c // 2, c % 2
        s0 = hf * HALF
        if hf == 0:
            seg = np.concatenate([np.zeros((halo, x.shape[2]), np.float32), x[b, 0:HALF]], axis=0)
        else:
            seg = x[b, s0 - halo:s0 + HALF]
        out.append(_fm(seg))
    return out


def _gather_tok(res, key="yT"):
    x = np.empty((4, SEQ, 2048), np.float32)
    for c in range(NCORE):
        b, hf = c // 2, c % 2
        y = np.asarray(res[c][key]).reshape(2048, HALF).T
        x[b, hf * HALF:(hf + 1) * HALF] = y
    return x


def _pvec(v, n):
    return np.ascontiguousarray(v.reshape(n, 128).T)


def kernel(x, mix_norm, ffn_norm, att_w_qkv, att_q_gain, att_k_gain, att_rel_bias, att_w_o, pool_w, pool_scale,
           gdn_w_in, gdn_conv, gdn_a_log, gdn_dt_bias, gdn_o_gain, gdn_w_o, ffn_w_up, ffn_conv, ffn_w_down):
    f32 = lambda a_: np.asarray(a_, dtype=np.float32)
    x = f32(x)
    DEPTH = 4
    to_cast = []
    names = []
    for l in range(DEPTH):
        _, _, _, wu, wd = ffn_layouts(np.zeros((2, 2048), np.float32), f32(ffn_norm[l]), f32(ffn_conv[l]),
                                      f32(ffn_w_up[l]), f32(ffn_w_down[l]))
        to_cast += [wu, wd]; names += [("ffn_up", l), ("ffn_dn", l)]
    for j in range(2):
        wq, wo_ = att_layouts(f32(att_w_qkv[j]), f32(att_w_o[j]))
        to_cast += [wq, wo_]; names += [("att_qkv", j), ("att_o", j)]
    to_cast.append(pool_layouts(f32(pool_w[0]))); names.append(("pool_w", 0))
    gl = []
    for r in range(2):
        win, wab, cw, hp = gdn1_layouts(f32(gdn_w_in[0]), f32(gdn_conv[0]), f32(gdn_a_log[0]), f32(gdn_dt_bias[0]), r)
        gl.append((cw, hp))
        to_cast += [win, wab]; names += [("gdn_in", r), ("gdn_ab", r)]
    gwo = f32(gdn_w_o[0]).reshape(32, 128, 4, 512).transpose(2, 1, 0, 3).reshape(4, 128, 32 * 512)
    to_cast.append(np.ascontiguousarray(gwo)); names.append(("gdn_o", 0))
    W = dict(zip(names, _cast_all(to_cast)))

    progs = {}

    def prog(name, fn):
        if name not in progs:
            progs[name] = fn()
        return progs[name]

    def run_ffn(x, l):
        nc = prog("ffn", lambda: build_ffn(HALF // 512))
        xs = _core_slices(x, 2)
        g = _pvec(f32(ffn_norm[l]), 16)
        c = np.ascontiguousarray(f32(ffn_conv[l]).reshape(3, 43, 128).transpose(2, 1, 0).reshape(128, 43 * 3))
        im = [{"xT": xs[k], "gain": g, "cw": c, "wup": W[("ffn_up", l)], "wdn": W[("ffn_dn", l)]} for k in range(NCORE)]
        return _gather_tok(_run(nc, im))

    def run_att(x, l, j):
        nc = prog("att", lambda: build_att(HALF // 512))
        xs = _core_slices(x, 512)
        g = _pvec(f32(mix_norm[l]), 16)
        qk = np.ascontiguousarray(np.stack([f32(att_q_gain[j]), f32(att_k_gain[j])], 1))
        bt = att_bias_table(f32(att_rel_bias[j])).astype(NPBF)
        im = []
        for k in range(NCORE):
            hmv = np.float32(-1e30) if k % 2 == 0 else np.float32(0.0)
            im.append({"xT": xs[k], "gain": g, "qkgain": qk, "hm": np.full((128, 1), hmv, np.float32), "bt": bt,
                       "wqkv": W[("att_qkv", j)], "wo": W[("att_o", j)]})
        return _gather_tok(_run(nc, im))

    def run_pool(x, l):
        nc = prog("pool", lambda: build_pool(HALF // 512))
        xs = _core_slices(x, 16)
        g = _pvec(f32(mix_norm[l]), 16)
        ps_ = _pvec(f32(pool_scale[0]), 16)
        im = [{"xT": xs[k], "gain": g, "pscale": ps_, "rc": pool_rc(k % 2 == 0), "pw": W[("pool_w", 0)]} for k in range(NCORE)]
        return _gather_tok(_run(nc, im))

    def run_gdn(x, l):
        nc1 = prog("gdn1", lambda: build_gdn1(SEQ // 512))
        g = _pvec(f32(mix_norm[l]), 16)
        cst = gdn_consts().reshape(128, 8 * 128)
        og_ = np.ascontiguousarray(f32(gdn_o_gain[0]).reshape(128, 1))
        im = []
        for k in range(NCORE):
            b, r = k // 2, k % 2
            im.append({"xT": _fm(x[b]), "gain": g, "cw": gl[r][0], "win": W[("gdn_in", r)], "wab": W[("gdn_ab", r)],
                       "hp": gl[r][1], "ogain": og_, "cst": cst})
        res = _run(nc1, im)
        nc2 = prog("gdn2", lambda: build_gdn2(HALF // 512))
        xs = _core_slices(x, 0)
        im2 = []
        for k in range(NCORE):
            b, hf = k // 2, k % 2
            og = np.concatenate([np.asarray(res[2 * b]["og"]), np.asarray(res[2 * b + 1]["og"])], axis=0)
            im2.append({"xT": xs[k], "ogT": np.ascontiguousarray(og[:, :, hf * HALF:(hf + 1) * HALF]), "wo": W[("gdn_o", 0)]})
        return _gather_tok(_run(nc2, im2))

    for i in range(DEPTH):
        kind, j = i % 3, i // 3
        if kind == 0:
            x = run_att(x, i, j)
        elif kind == 1:
            x = run_pool(x, i)
        else:
            x = run_gdn(x, i)
        x = run_ffn(x, i)
    return x
```
